# Optimizing a Trainium2 kernel written in Bass

```python
import jax, jax.numpy as jnp
from jax import lax
import numpy as np

D_MODEL = 2048
BATCH = 4
SEQ = 4096
DEPTH = 2

PLE_DIM = 256
D_FF = 4 * D_MODEL
EPS = 1e-6

SB_HEADS = 8
SB_HEAD_DIM = D_MODEL // (2 * SB_HEADS)
SB_BLOCK = 128
SB_W = SB_HEADS * SB_HEAD_DIM
GLA_HEADS = 8
GLA_DV = D_MODEL // (2 * GLA_HEADS)
GLA_DK = GLA_DV // 2
GLA_GATE_RANK = 16
GLA_TAU = 16.0
GLA_CHUNK = 64
GLA_K = GLA_HEADS * GLA_DK
GLA_V = GLA_HEADS * GLA_DV
AB_SPLITS = [SB_W, 2 * SB_W, 3 * SB_W, 3 * SB_W + GLA_K, 3 * SB_W + 2 * GLA_K,
             3 * SB_W + 2 * GLA_K + GLA_V, 3 * SB_W + 2 * GLA_K + GLA_V + GLA_GATE_RANK]
AB_IN = 3 * SB_W + 2 * GLA_K + 2 * GLA_V + GLA_GATE_RANK
AB_OUT = SB_W + GLA_V
SSD_INNER = 2 * D_MODEL
SSD_HEAD_DIM = 64
SSD_HEADS = SSD_INNER // SSD_HEAD_DIM
SSD_GROUPS = 8
SSD_STATE = 128
SSD_CONV = 4
SSD_CHUNK = 128
SSD_BC = SSD_GROUPS * SSD_STATE
SSD_CONV_DIM = SSD_INNER + 2 * SSD_BC
SSD_IN = SSD_INNER + SSD_CONV_DIM + SSD_HEADS

kernel_name = "hybrid_stickbreak_gla_ssd_block"

F32 = jnp.float32


def rms_norm(x, g):
    xf = x.astype(F32)
    y = xf * lax.rsqrt(jnp.mean(xf * xf, axis=-1, keepdims=True) + EPS)
    return (y * g.astype(F32)).astype(x.dtype)


def stick_breaking_attention(q, k, v):
    b, h, s, d = q.shape
    nb = s // SB_BLOCK
    scale = d ** -0.5
    kf = k.astype(F32)
    vf = v.astype(F32)
    qb = q.astype(F32).reshape(b, h, nb, SB_BLOCK, d).transpose(2, 0, 1, 3, 4)
    key_pos = jnp.arange(s)

    def block(args):
        qi, i = args
        z = jnp.einsum('bhtd,bhsd->bhts', qi, kf) * scale
        q_pos = i * SB_BLOCK + jnp.arange(SB_BLOCK)
        mask = key_pos[None, :] < q_pos[:, None]
        log_beta = jax.nn.log_sigmoid(z)
        log_1m = jnp.where(mask, log_beta - z, 0.0)
        after = lax.cumsum(log_1m, axis=3, reverse=True) - log_1m
        w = jnp.where(mask, jnp.exp(log_beta + after), 0.0)
        return jnp.einsum('bhts,bhsd->bhtd', w, vf)

    out = lax.map(block, (qb, jnp.arange(nb)))
    return out.transpose(1, 2, 0, 3, 4).reshape(b, h, s, d)


def gla_chunked(q, k, v, log_a):
    b, h, s, dk = q.shape
    dv = v.shape[-1]
    c = GLA_CHUNK
    nc = s // c
    f = lambda t: t.astype(F32).reshape(b, h, nc, c, t.shape[-1])
    q = f(q) * dk ** -0.5
    k = f(k)
    v = f(v)
    gcum = jnp.cumsum(f(log_a), axis=3)
    g_last = gcum[:, :, :, -1:, :]
    q_dec = q * jnp.exp(gcum)
    k_inv = k * jnp.exp(-gcum)
    k_end = k * jnp.exp(g_last - gcum)
    causal = jnp.tril(jnp.ones((c, c), dtype=bool))
    scores = jnp.where(causal, jnp.einsum('bhntd,bhnsd->bhnts', q_dec, k_inv), 0.0)
    o_intra = jnp.einsum('bhnts,bhnse->bhnte', scores, v)
    d_state = jnp.einsum('bhnsd,bhnse->nbhde', k_end, v)
    chunk_decay = jnp.exp(g_last[:, :, :, 0, :]).transpose(2, 0, 1, 3)

    def step(state, inp):
        dec, ds = inp
        return state * dec[..., None] + ds, state

    _, prev = lax.scan(step, jnp.zeros((b, h, dk, dv), F32), (chunk_decay, d_state))
    o_inter = jnp.einsum('bhntd,nbhde->bhnte', q_dec, prev)
    return (o_intra + o_inter).reshape(b, h, s, dv)


def sb_gla_mixer(x, w_in, w_gate_up, b_gate, gla_norm, w_out):
    b, s, _ = x.shape
    proj = x @ w_in
    sq, sk, sv, gq, gk, gv, glr, gout = jnp.split(proj, AB_SPLITS, axis=-1)
    heads = lambda t, n: t.reshape(b, s, n, -1).transpose(0, 2, 1, 3)
    o_sb = stick_breaking_attention(heads(sq, SB_HEADS), heads(sk, SB_HEADS), heads(sv, SB_HEADS))
    log_a = jax.nn.log_sigmoid((glr @ w_gate_up + b_gate).astype(F32)) / GLA_TAU
    o_gla = gla_chunked(heads(gq, GLA_HEADS), heads(gk, GLA_HEADS), heads(gv, GLA_HEADS),
                        heads(log_a, GLA_HEADS))
    o_gla = rms_norm(o_gla.transpose(0, 2, 1, 3), gla_norm) * \
        jax.nn.silu(gout.astype(F32)).reshape(b, s, GLA_HEADS, GLA_DV)
    o = jnp.concatenate([o_sb.transpose(0, 2, 1, 3).reshape(b, s, SB_W),
                         o_gla.reshape(b, s, GLA_V)], axis=-1).astype(x.dtype)
    return o @ w_out


def ssd_chunked(x, dt, a, bm, cm):
    b, s, nh, hp = x.shape
    g, n = bm.shape[2], bm.shape[3]
    r = nh // g
    c = SSD_CHUNK
    nc = s // c
    xdt = (x.astype(F32) * dt[..., None]).reshape(b, nc, c, g, r, hp)
    da = (dt * a).reshape(b, nc, c, g, r).transpose(0, 3, 4, 1, 2)
    bm = bm.astype(F32).reshape(b, nc, c, g, n)
    cm = cm.astype(F32).reshape(b, nc, c, g, n)
    a_cs = jnp.cumsum(da, axis=-1)
    causal = jnp.tril(jnp.ones((c, c), dtype=bool))
    seg = a_cs[..., :, None] - a_cs[..., None, :]
    L = jnp.exp(jnp.where(causal, seg, -jnp.inf))
    cb = jnp.einsum('bnlgk,bnsgk->bgnls', cm, bm)
    y_diag = jnp.einsum('bgnls,bgrnls,bnsgrp->bnlgrp', cb, L, xdt)
    decay_states = jnp.exp(a_cs[..., -1:] - a_cs)
    states = jnp.einsum('bnsgk,bgrns,bnsgrp->nbgrpk', bm, decay_states, xdt)
    chunk_decay = jnp.exp(a_cs[..., -1]).transpose(3, 0, 1, 2)

    def step(state, inp):
        dec, st = inp
        return state * dec[..., None, None] + st, state

    _, prev = lax.scan(step, jnp.zeros((b, g, r, hp, n), F32), (chunk_decay, states))
    y_off = jnp.einsum('bnlgk,nbgrpk,bgrnl->bnlgrp', cm, prev, jnp.exp(a_cs))
    return (y_diag + y_off).reshape(b, s, nh, hp)


def ssd_mixer(x, w_in, conv_w, conv_b, dt_bias, a_log, d_skip, norm_g, w_out):
    b, s, _ = x.shape
    proj = x @ w_in
    z, xbc, dt = jnp.split(proj, [SSD_INNER, SSD_INNER + SSD_CONV_DIM], axis=-1)
    xbc = lax.conv_general_dilated(xbc, conv_w[:, None, :], window_strides=(1,),
                                   padding=[(SSD_CONV - 1, 0)],
                                   dimension_numbers=('NWC', 'WIO', 'NWC'),
                                   feature_group_count=SSD_CONV_DIM) + conv_b
    xbc = jax.nn.silu(xbc)
    xs, bm, cm = jnp.split(xbc, [SSD_INNER, SSD_INNER + SSD_BC], axis=-1)
    dt = jax.nn.softplus(dt.astype(F32) + dt_bias.astype(F32))
    a = -jnp.exp(a_log.astype(F32))
    xs = xs.reshape(b, s, SSD_HEADS, SSD_HEAD_DIM)
    y = ssd_chunked(xs, dt, a, bm.reshape(b, s, SSD_GROUPS, SSD_STATE),
                    cm.reshape(b, s, SSD_GROUPS, SSD_STATE))
    y = y + d_skip.astype(F32)[:, None] * xs.astype(F32)
    y = y.reshape(b, s, SSD_INNER) * jax.nn.silu(z.astype(F32))
    y = rms_norm(y, norm_g).astype(x.dtype)
    return y @ w_out


def squared_relu_mlp(x, w_up, w_down):
    return jnp.square(jax.nn.relu(x @ w_up)) @ w_down


def setup_inputs(seed: int = 0) -> dict:
    key = jax.random.key(seed)
    ks = iter(jax.random.split(key, 32))
    n_even = (DEPTH + 1) // 2
    n_odd = DEPTH // 2

    def dense(shape, fan_in):
        return jax.random.normal(next(ks), shape, F32) * fan_in ** -0.5

    def gain(shape):
        return 1.0 + 0.02 * jax.random.normal(next(ks), shape, F32)

    def small(shape):
        return 0.01 * jax.random.normal(next(ks), shape, F32)

    x = jax.random.normal(next(ks), (BATCH, SEQ, D_MODEL), F32)
    p = jax.random.normal(next(ks), (DEPTH, BATCH, SEQ, PLE_DIM), F32)
    dt0 = jnp.exp(jax.random.uniform(next(ks), (n_odd, SSD_HEADS), F32,
                                     minval=float(np.log(1e-3)), maxval=float(np.log(1e-1))))
    ssd_dt_bias = dt0 + jnp.log(-jnp.expm1(-dt0))
    ssd_a_log = jnp.log(jax.random.uniform(next(ks), (n_odd, SSD_HEADS), F32, minval=1.0, maxval=16.0))
    return {
        "x": x,
        "p": p,
        "norm_mix": gain((DEPTH, D_MODEL)),
        "norm_mlp": gain((DEPTH, D_MODEL)),
        "ab_w_in": dense((n_even, D_MODEL, AB_IN), D_MODEL),
        "ab_w_gate_up": dense((n_even, GLA_GATE_RANK, GLA_K), GLA_GATE_RANK),
        "ab_b_gate": small((n_even, GLA_K)),
        "ab_gla_norm": gain((n_even, GLA_DV)),
        "ab_w_out": dense((n_even, AB_OUT, D_MODEL), AB_OUT),
        "ssd_w_in": dense((n_odd, D_MODEL, SSD_IN), D_MODEL),
        "ssd_conv_w": dense((n_odd, SSD_CONV, SSD_CONV_DIM), SSD_CONV),
        "ssd_conv_b": small((n_odd, SSD_CONV_DIM)),
        "ssd_dt_bias": ssd_dt_bias,
        "ssd_a_log": ssd_a_log,
        "ssd_d": gain((n_odd, SSD_HEADS)),
        "ssd_norm": gain((n_odd, SSD_INNER)),
        "ssd_w_out": dense((n_odd, SSD_INNER, D_MODEL), SSD_INNER),
        "mlp_w_up": dense((DEPTH, D_MODEL, D_FF), D_MODEL),
        "mlp_w_down": dense((DEPTH, D_FF, D_MODEL), D_FF),
        "ple_w_proj": dense((DEPTH, PLE_DIM, D_MODEL), PLE_DIM),
        "ple_w_gate": dense((DEPTH, D_MODEL, D_MODEL), D_MODEL),
        "final_norm": gain((D_MODEL,)),
    }


def reference(x, p, norm_mix, norm_mlp, ab_w_in, ab_w_gate_up, ab_b_gate, ab_gla_norm, ab_w_out,
              ssd_w_in, ssd_conv_w, ssd_conv_b, ssd_dt_bias, ssd_a_log, ssd_d, ssd_norm, ssd_w_out,
              mlp_w_up, mlp_w_down, ple_w_proj, ple_w_gate, final_norm):
    h = x
    for i in range(DEPTH):
        hn = rms_norm(h, norm_mix[i])
        j = i // 2
        if i % 2 == 0:
            h = h + sb_gla_mixer(hn, ab_w_in[j], ab_w_gate_up[j], ab_b_gate[j], ab_gla_norm[j],
                                 ab_w_out[j])
        else:
            h = h + ssd_mixer(hn, ssd_w_in[j], ssd_conv_w[j], ssd_conv_b[j], ssd_dt_bias[j],
                              ssd_a_log[j], ssd_d[j], ssd_norm[j], ssd_w_out[j])
        h = h + squared_relu_mlp(rms_norm(h, norm_mlp[i]), mlp_w_up[i], mlp_w_down[i])
        h = h + jax.nn.sigmoid(h @ ple_w_gate[i]) * (p[i] @ ple_w_proj[i])
    return rms_norm(h, final_norm)
```

```python
import contextlib
import numpy as np
import ml_dtypes
import concourse.bass as bass
import concourse.mybir as mybir
from concourse.bass_utils import run_bass_kernel_spmd

F32 = mybir.dt.float32
BF16 = mybir.dt.bfloat16
AF = mybir.ActivationFunctionType
ALU = mybir.AluOpType
AX = mybir.AxisListType
NPBF16 = ml_dtypes.bfloat16


def _flat(x):
    out = []
    for d in x:
        if isinstance(d, (list, tuple)):
            out.extend(_flat(d))
        elif d is not None:
            out.append(d)
    return out


class Dep:
    __slots__ = ("w", "r", "dsem", "dcount", "name", "excl")

    def __init__(self, name="", excl=False):
        self.excl = excl
        self.w = None
        self.r = {}
        self.dsem = None
        self.dcount = 0
        self.name = name


class Prog:
    ENGS = ("pe", "act", "dve", "pool", "sp")

    def __init__(self, nc, same_engine_sync=True):
        self.nc = nc
        self.stack = contextlib.ExitStack()
        self.root = self.stack
        self.sems = {}
        for e in self.ENGS:
            self.sems[e] = self.stack.enter_context(nc.semaphore(f"sem_{e}"))
        self.cnt = {e: 0 for e in self.ENGS}
        self.streams = {e: [] for e in self.ENGS}
        self.waited = {e: {} for e in self.ENGS}
        self.same_engine_sync = same_engine_sync
        self.final = []
        self.dma_latest = {}
        self.nsem = 0
        self.n_ops = 0

    def sb(self, name, shape, dtype):
        return self.stack.enter_context(self.nc.sbuf_tensor(self._nm("sb_" + name), list(shape), dtype))

    def ps(self, name, shape, dtype=F32):
        return self.stack.enter_context(self.nc.psum_tensor(self._nm("ps_" + name), list(shape), dtype))

    def _nm(self, name):
        self.n_names = getattr(self, "n_names", 0) + 1
        return f"{name}_{self.n_names}"

    def dep(self, name="", excl=False):
        return Dep(name, excl)

    def deps(self, n, name="", excl=False):
        return [Dep(f"{name}{i}", excl) for i in range(n)]

    def _newsem(self, name):
        self.nsem += 1
        return self.root.enter_context(self.nc.semaphore(f"dsem_{self.nsem}_{name}"))

    def _collect(self, eng, reads, writes):
        toks = {}

        def add(t):
            if t is None:
                return
            sem, val, src = t
            if src == eng and (eng == "pe" or not self.same_engine_sync):
                return
            k = id(sem)
            if k not in toks or toks[k][1] < val:
                toks[k] = t

        for d in reads:
            add(d.w)
        for d in writes:
            add(d.w)
            for t in d.r.values():
                add(t)
        out = []
        wd = self.waited[eng]
        for k, (sem, val, src) in toks.items():
            if wd.get(k, 0) >= val:
                continue
            wd[k] = val
            out.append((sem, val))
        return out

    def _record(self, tok, reads, writes):
        k = id(tok[0])
        for d in reads:
            if d in writes:
                continue
            o = d.r.get(k)
            if o is None or o[1] < tok[1]:
                d.r[k] = tok
        for d in writes:
            d.w = tok
            d.r = {}

    def op(self, eng, fn, reads=(), writes=(), inc=True):
        import os
        if self.n_ops >= int(os.environ.get("MAXOPS", "100000000")):
            return None
        reads = _flat(reads)
        writes = _flat(writes)
        for d in reads:
            if d.excl and d not in writes:
                writes.append(d)
        waits = self._collect(eng, reads, writes)
        if inc:
            self.cnt[eng] += 1
            tok = (self.sems[eng], self.cnt[eng], eng)
        else:
            tok = (self.sems[eng], self.cnt[eng] + 1, eng)
        self.streams[eng].append((waits, fn, ("inc", None) if inc else None))
        self._record(tok, reads, writes)
        self.n_ops += 1
        return tok

    def dma(self, out, in_, reads=(), writes=(), queue="sp", final=False, n=1, **kw):
        import os
        if self.n_ops >= int(os.environ.get("MAXOPS", "100000000")):
            return None
        reads = _flat(reads)
        writes = _flat(writes)
        target = writes[0] if writes else reads[0]
        if target.dsem is None:
            target.dsem = self._newsem(target.name)
        waits = self._collect(queue, reads, writes)
        target.dcount += 16
        tok = (target.dsem, target.dcount, None)
        self.dma_latest[id(target.dsem)] = tok
        self.streams[queue].append(
            (waits, lambda e: e.dma_start(out=out, in_=in_, **kw), ("dma", target.dsem)))
        self._record(tok, reads, writes)
        if final:
            self.final.append(tok)
        self.n_ops += 1
        return tok

    def barrier(self):
        toks = [(self.sems[e], self.cnt[e]) for e in self.ENGS if self.cnt[e] > 0]
        toks += [(t[0], t[1]) for t in self.dma_latest.values()]
        for e in self.ENGS:
            wd = self.waited[e]
            waits = []
            for sem, val in toks:
                if sem is self.sems[e]:
                    continue
                if wd.get(id(sem), 0) >= val:
                    continue
                wd[id(sem)] = val
                waits.append((sem, val))
            self.streams[e].append((waits, None, None))

    @contextlib.contextmanager
    def scope(self):
        outer = self.stack
        self.stack = contextlib.ExitStack()
        try:
            yield
        finally:
            self.barrier()
            self.stack.close()
            self.stack = outer

    def emit(self):
        nc = self.nc
        fin = {}
        for sem, val, _ in self.final:
            k = id(sem)
            if k not in fin or fin[k][1] < val:
                fin[k] = (sem, val)
        self.streams["sp"].append((list(fin.values()), None, None))
        sems = self.sems

        def run(eng_name):
            def body(e):
                for waits, fn, kind in self.streams[eng_name]:
                    for sem, val in waits:
                        e.wait_ge(sem, val)
                    if fn is None:
                        continue
                    ins = fn(e)
                    if kind is None:
                        continue
                    if kind[0] == "inc":
                        ins.then_inc(sems[eng_name], 1)
                    else:
                        ins.then_inc(kind[1], 16)
            return body

        with nc.Block() as block:
            block.tensor(run("pe"))
            block.scalar(run("act"))
            block.vector(run("dve"))
            block.gpsimd(run("pool"))
            block.sync(run("sp"))
        self.stack.close()


D = 2048
EPS = 1e-6
TT = 512
KC = D // 128


def new_nc():
    return bass.Bass("TRN2", target_bir_lowering=False)


def din(nc, name, shape, dt):
    return nc.dram_tensor(name, list(shape), dt, kind="ExternalInput").ap()


def dout(nc, name, shape, dt):
    return nc.dram_tensor(name, list(shape), dt, kind="ExternalOutput").ap()


class Common:
    def __init__(self, P, ident_ap=None):
        self.P = P
        nc = P.nc
        self.ones_bf = P.sb("ones_bf", [128, 128], BF16)
        self.d_ones = P.dep("ones")
        P.op("pool", lambda e: e.memset(self.ones_bf[:], 1.0), writes=[self.d_ones])
        self.eps = P.sb("eps_t", [128, 1], F32)
        self.d_eps = P.dep("eps")
        P.op("pool", lambda e: e.memset(self.eps[:], EPS), writes=[self.d_eps])
        if ident_ap is not None:
            self.ident = P.sb("ident", [128, 128], F32)
            self.d_ident = P.dep("ident")
            P.dma(self.ident[:], ident_ap, writes=[self.d_ident])


def emit_rmsnorm_fm(P, C, h, d_h, g, d_g, out, d_out, sq, d_sq, ps_ss, d_ss, rstd, d_rstd, nt=TT, nchunk=KC, dim=D):
    P.op("act", lambda e: e.activation(out=sq[:], in_=h[:], func=AF.Square), reads=[d_h], writes=[d_sq])
    for c in range(nchunk):
        P.op("pe", lambda e, c=c: e.matmul(ps_ss[:], lhsT=C.ones_bf[:], rhs=sq[:, c, :], start=(c == 0), stop=(c == nchunk - 1)),
             reads=[d_sq, C.d_ones], writes=[d_ss], inc=(c == nchunk - 1))
    P.op("act", lambda e: e.activation(out=rstd[:], in_=ps_ss[:], func=AF.Sqrt, bias=C.eps[:], scale=1.0 / dim),
         reads=[d_ss, C.d_eps], writes=[d_rstd])
    P.op("dve", lambda e: e.reciprocal(out=rstd[:], in_=rstd[:]), reads=[d_rstd], writes=[d_rstd])
    for c in range(nchunk):
        P.op("dve", lambda e, c=c: e.scalar_tensor_tensor(out=out[:, c, :], in0=h[:, c, :], scalar=g[:, c:c + 1], in1=rstd[:],
                                                           op0=ALU.mult, op1=ALU.mult),
             reads=[d_h, d_g, d_rstd], writes=[d_out])


def emit_cast(P, src, dst, M, CH=4096):
    nbuf = 3
    st = [P.sb(f"cst{i}", [128, CH], F32) for i in range(nbuf)]
    ob = [P.sb(f"cob{i}", [128, CH], BF16) for i in range(nbuf)]
    dst_ = [P.dep(f"cst{i}") for i in range(nbuf)]
    dob = [P.dep(f"cob{i}") for i in range(nbuf)]
    engs = ["dve", "act", "pool"]
    n = (M + CH - 1) // CH
    for i in range(n):
        b = i % nbuf
        w = min(CH, M - i * CH)
        P.dma(st[b][:, :w], src[:, i * CH:i * CH + w], writes=[dst_[b]])
        eng = engs[i % 3]
        if eng == "act":
            P.op("act", lambda e, b=b, w=w: e.copy(out=ob[b][:, :w], in_=st[b][:, :w]), reads=[dst_[b]], writes=[dob[b]])
        else:
            P.op(eng, lambda e, b=b, w=w: e.tensor_copy(out=ob[b][:, :w], in_=st[b][:, :w]), reads=[dst_[b]], writes=[dob[b]])
        P.dma(dst[:, i * CH:i * CH + w], ob[b][:, :w], reads=[dob[b]], final=True)


def build_cast(M):
    nc = new_nc()
    src = din(nc, "w32", [128, M], F32)
    dst = dout(nc, "w16", [128, M], BF16)
    P = Prog(nc)
    emit_cast(P, src, dst, M)
    P.emit()
    return nc


def emit_p0(P, C, x, g_ap, xT, hnT, NT):
    g = P.sb("g0", [128, KC], F32)
    d_g = P.dep("g0")
    P.dma(g[:], g_ap, writes=[d_g])
    xin = [P.sb(f"xin{i}", [128, D], F32) for i in range(2)]
    d_xin = P.deps(2, "xin")
    pst = [P.ps(f"pst{i}", [128, 512], F32) for i in range(2)]
    d_pst = P.deps(2, "pst", excl=True)
    hT = [P.sb(f"hT{i}", [128, KC, TT], F32) for i in range(2)]
    d_hT = P.deps(2, "hT")
    hb = [P.sb(f"hb{i}", [128, KC, TT], BF16) for i in range(2)]
    d_hb = P.deps(2, "hb")
    sq = P.sb("sq", [128, KC, TT], BF16)
    d_sq = P.dep("sq")
    ps_ss = P.ps("ps_ss", [128, TT], F32)
    d_ss = P.dep("ss", excl=True)
    rstd = P.sb("rstd", [128, TT], F32)
    d_rstd = P.dep("rstd")
    ntile = NT // TT
    k = 0
    ev = 0
    for t in range(ntile):
        tb = t % 2
        for s in range(TT // 128):
            b = k % 2
            k += 1
            r0 = t * TT + s * 128
            P.dma(xin[b][:], x[r0:r0 + 128, :], writes=[d_xin[b]])
            for cg in range(KC // 4):
                pb = ev % 2
                for j in range(4):
                    c = cg * 4 + j
                    P.op("pe", lambda e, b=b, c=c, pb=pb, j=j: e.transpose(out=pst[pb][:, j * 128:(j + 1) * 128],
                                                                         in_=xin[b][:, c * 128:(c + 1) * 128], identity=C.ident[:]),
                         reads=[d_xin[b], C.d_ident], writes=[d_pst[pb]], inc=(j == 3))
                eng = "dve" if ev % 2 == 0 else "act"
                src = pst[pb][:].rearrange("p (j t) -> p j t", j=4)
                dst = hT[tb][:, cg * 4:cg * 4 + 4, s * 128:(s + 1) * 128]
                if eng == "dve":
                    P.op("dve", lambda e, src=src, dst=dst: e.tensor_copy(out=dst, in_=src), reads=[d_pst[pb]], writes=[d_hT[tb]])
                else:
                    P.op("act", lambda e, src=src, dst=dst: e.copy(out=dst, in_=src), reads=[d_pst[pb]], writes=[d_hT[tb]])
                ev += 1
        emit_rmsnorm_fm(P, C, hT[tb], d_hT[tb], g, d_g, hb[tb], d_hb[tb], sq, d_sq, ps_ss, d_ss, rstd, d_rstd)
        c0 = t * TT
        P.dma(xT[:, :, c0:c0 + TT].rearrange("c p t -> p c t"), hT[tb][:], reads=[d_hT[tb]], final=True)
        P.dma(hnT[:, :, c0:c0 + TT].rearrange("c p t -> p c t"), hb[tb][:], reads=[d_hb[tb]], final=True)


def build_p0(NT):
    nc = new_nc()
    x = din(nc, "x", [NT, D], F32)
    g = din(nc, "g", [128, KC], F32)
    ident = din(nc, "ident", [128, 128], F32)
    xT = dout(nc, "xT", [KC, 128, NT], F32)
    hnT = dout(nc, "hnT", [KC, 128, NT], BF16)
    P = Prog(nc)
    C = Common(P, ident)
    emit_p0(P, C, x, g, xT, hnT, NT)
    P.emit()
    return nc


class WRing:
    def __init__(self, P, n, name="wr"):
        self.P = P
        self.bufs = [P.sb(f"{name}{i}", [128, 8192], BF16) for i in range(n)]
        self.deps = P.deps(n, name)
        self.i = 0

    def load(self, src):
        s = self.i % len(self.bufs)
        self.i += 1
        self.P.dma(self.bufs[s][:], src, writes=[self.deps[s]])
        return self.bufs[s], self.deps[s]


def emit_r(P, C, A, NT, KO, last):
    Co = 8192 // KO
    g_mlp = P.sb("g_mlp", [128, KC], F32); d_gm = P.dep("g_mlp")
    g_nxt = P.sb("g_nxt", [128, KC], F32); d_gn = P.dep("g_nxt")
    P.dma(g_mlp[:], A["g_mlp"], writes=[d_gm])
    P.dma(g_nxt[:], A["g_nxt"], writes=[d_gn])
    wpj = P.sb("wpj", [128, 2, D], BF16); d_wpj = P.dep("wpj")
    P.dma(wpj[:], A["wpj"].rearrange("p (k n) -> p k n", k=2), writes=[d_wpj])
    h = P.sb("h", [128, KC, TT], F32); d_h = P.dep("h")
    hb = P.sb("hb", [128, KC, TT], BF16); d_hb = P.dep("hb")
    ubuf = P.sb("ubuf", [128, 64 * TT], BF16); d_q = P.deps(4, "uq")
    u = ubuf[:].rearrange("p (j t) -> p j t", t=TT)
    oT = ubuf[:, 0:KO * TT].rearrange("p (j t) -> p j t", t=TT)
    d_oT = d_q[0:KO // 16]
    sq = ubuf[:, 32 * TT:48 * TT].rearrange("p (j t) -> p j t", t=TT); d_sq = d_q[2]
    ring = WRing(P, 3)
    acc = [P.ps(f"acc{i}", [128, TT], F32) for i in range(4)]; d_acc = P.deps(4, "acc", excl=True)
    ps_ss = P.ps("ss", [128, TT], F32); d_ss = P.dep("ss", excl=True)
    rstd = P.sb("rstd", [128, TT], F32); d_rstd = P.dep("rstd")
    tmp = [P.sb(f"tmp{i}", [128, TT], F32) for i in range(3)]; d_tmp = P.deps(3, "tmp")
    p32 = P.sb("p32", [128, 2, TT], F32); d_p32 = P.dep("p32")
    pbf = P.sb("pbf", [128, 2, TT], BF16); d_pbf = P.dep("pbf")
    if KO == 32:
        ssin = P.sb("ssin", [2, TT], F32); d_ssin = P.dep("ssin")
        ones2 = P.sb("ones2", [2, 128], F32); d_ones2 = P.dep("ones2")
        P.op("pool", lambda e: e.memset(ones2[:], 1.0), writes=[d_ones2])
        rs2 = P.sb("rs2", [128, TT], F32); d_rs2 = P.dep("rs2")
    if last:
        hn32 = ubuf[:, 0:32 * TT].bitcast(F32).rearrange("p (j t) -> p j t", t=TT)
        d_hn32 = [d_q[0], d_q[1]]
        ostg = [ubuf[:, 48 * TT + i * 4096:48 * TT + (i + 1) * 4096].bitcast(F32) for i in range(2)]
        d_ostg = P.deps(2, "ostg")
        pst = [P.ps(f"pst{i}", [128, 512], F32) for i in range(2)]; d_pst = P.deps(2, "pst", excl=True)
    ai = [0]

    def next_acc():
        i = ai[0] % 4
        ai[0] += 1
        return acc[i], d_acc[i]

    for t in range(NT // TT):
        c0 = t * TT
        P.dma(h[:], A["hT"][:, :, c0:c0 + TT].rearrange("c p t -> p c t"), writes=[d_h])
        P.dma(oT, A["oT"][:, :, c0:c0 + TT].rearrange("c p t -> p c t"), writes=d_oT)
        P.dma(p32[:], A["pT"][:, :, c0:c0 + TT].rearrange("c p t -> p c t"), writes=[d_p32])
        P.op("pool", lambda e: e.tensor_copy(out=pbf[:], in_=p32[:]), reads=[d_p32], writes=[d_pbf])
        if KO == 32:
            P.dma(ssin[:], A["ss"][:, c0:c0 + TT], writes=[d_ssin])
            P.op("pe", lambda e: e.matmul(ps_ss[:], lhsT=ones2[:], rhs=ssin[:], start=True, stop=True),
                 reads=[d_ssin, d_ones2], writes=[d_ss])
            P.op("act", lambda e: e.activation(out=rs2[:], in_=ps_ss[:], func=AF.Sqrt, bias=C.eps[:], scale=1.0 / 4096),
                 reads=[d_ss, C.d_eps], writes=[d_rs2])
            P.op("dve", lambda e: e.reciprocal(out=rs2[:], in_=rs2[:]), reads=[d_rs2], writes=[d_rs2])
            P.op("dve", lambda e: e.tensor_tensor(out=oT, in0=oT, in1=rs2[:].unsqueeze(1).to_broadcast([128, KO, TT]), op=ALU.mult),
                 reads=[d_rs2] + d_oT, writes=d_oT)
        pan = None
        for nch in range(KC):
            pi, off = divmod(nch * 128, Co)
            if off == 0:
                pan, d_pan = ring.load(A["wo"][pi])
                panv = pan[:].rearrange("p (k c) -> p k c", c=Co)
            a, d_a = next_acc()
            for k in range(KO):
                P.op("pe", lambda e, a=a, panv=panv, k=k, off=off: e.matmul(a[:], lhsT=panv[:, k, off:off + 128], rhs=oT[:, k, :],
                                                                         start=(k == 0), stop=(k == KO - 1)),
                     reads=[d_pan] + d_oT, writes=[d_a], inc=(k == KO - 1))
            P.op("dve", lambda e, a=a, nch=nch: e.tensor_tensor(out=h[:, nch, :], in0=a[:], in1=h[:, nch, :], op=ALU.add),
                 reads=[d_a, d_h], writes=[d_h])
        emit_rmsnorm_fm(P, C, h, d_h, g_mlp, d_gm, hb, d_hb, sq, d_sq, ps_ss, d_ss, rstd, d_rstd)
        for j in range(64):
            if j % 4 == 0:
                pan, d_pan = ring.load(A["wup"][j // 4])
                panv = pan[:].rearrange("p (k c) -> p k c", c=512)
            off = (j % 4) * 128
            a, d_a = next_acc()
            for k in range(KC):
                P.op("pe", lambda e, a=a, panv=panv, k=k, off=off: e.matmul(a[:], lhsT=panv[:, k, off:off + 128], rhs=hb[:, k, :],
                                                                         start=(k == 0), stop=(k == KC - 1)),
                     reads=[d_pan, d_hb], writes=[d_a], inc=(k == KC - 1))
            tb = j % 3
            P.op("act", lambda e, a=a, tb=tb: e.activation(out=tmp[tb][:], in_=a[:], func=AF.Relu),
                 reads=[d_a], writes=[d_tmp[tb]])
            P.op("pool", lambda e, j=j, tb=tb: e.tensor_tensor(out=u[:, j, :], in0=tmp[tb][:], in1=tmp[tb][:], op=ALU.mult),
                 reads=[d_tmp[tb]], writes=[d_q[j // 16]])
        for nch in range(KC):
            pan, d_pan = ring.load(A["wdn"][nch])
            panv = pan[:].rearrange("p (k c) -> p k c", c=128)
            a, d_a = next_acc()
            for j in range(64):
                P.op("pe", lambda e, a=a, panv=panv, j=j: e.matmul(a[:], lhsT=panv[:, j, :], rhs=u[:, j, :],
                                                                 start=(j == 0), stop=(j == 63)),
                     reads=[d_pan, d_q[j // 16]], writes=[d_a], inc=(j == 63))
            P.op("dve", lambda e, a=a, nch=nch: e.tensor_tensor(out=h[:, nch, :], in0=a[:], in1=h[:, nch, :], op=ALU.add),
                 reads=[d_a, d_h], writes=[d_h])
        P.op("act", lambda e: e.copy(out=hb[:], in_=h[:]), reads=[d_h], writes=[d_hb])
        for nch in range(KC):
            if nch % 4 == 0:
                pan, d_pan = ring.load(A["wg"][nch // 4])
                panv = pan[:].rearrange("p (k c) -> p k c", c=512)
            off = (nch % 4) * 128
            a, d_a = next_acc()
            b, d_b = next_acc()
            for k in range(KC):
                P.op("pe", lambda e, a=a, panv=panv, k=k, off=off: e.matmul(a[:], lhsT=panv[:, k, off:off + 128], rhs=hb[:, k, :],
                                                                         start=(k == 0), stop=(k == KC - 1)),
                     reads=[d_pan, d_hb], writes=[d_a], inc=(k == KC - 1))
            for k in range(2):
                P.op("pe", lambda e, b=b, k=k, nch=nch: e.matmul(b[:], lhsT=wpj[:, k, nch * 128:(nch + 1) * 128], rhs=pbf[:, k, :],
                                                               start=(k == 0), stop=(k == 1)),
                     reads=[d_wpj, d_pbf], writes=[d_b], inc=(k == 1))
            tb = nch % 3
            P.op("act", lambda e, a=a, tb=tb: e.activation(out=tmp[tb][:], in_=a[:], func=AF.Sigmoid),
                 reads=[d_a], writes=[d_tmp[tb]])
            P.op("dve", lambda e, b=b, tb=tb: e.tensor_tensor(out=tmp[tb][:], in0=b[:], in1=tmp[tb][:], op=ALU.mult),
                 reads=[d_b, d_tmp[tb]], writes=[d_tmp[tb]])
            P.op("pool", lambda e, nch=nch, tb=tb: e.tensor_tensor(out=h[:, nch, :], in0=h[:, nch, :], in1=tmp[tb][:], op=ALU.add),
                 reads=[d_tmp[tb], d_h], writes=[d_h])
        if not last:
            P.dma(A["hT_out"][:, :, c0:c0 + TT].rearrange("c p t -> p c t"), h[:], reads=[d_h], final=True, queue="act")
            emit_rmsnorm_fm(P, C, h, d_h, g_nxt, d_gn, hb, d_hb, sq, d_sq, ps_ss, d_ss, rstd, d_rstd)
            P.dma(A["hnT_out"][:, :, c0:c0 + TT].rearrange("c p t -> p c t"), hb[:], reads=[d_hb], final=True, queue="act")
        else:
            emit_rmsnorm_fm(P, C, h, d_h, g_nxt, d_gn, hn32, d_hn32, sq, d_sq, ps_ss, d_ss, rstd, d_rstd)
            ev = 0
            for s in range(TT // 128):
                ob = s % 2
                for cg in range(KC // 4):
                    pb = ev % 2
                    for j in range(4):
                        c = cg * 4 + j
                        P.op("pe", lambda e, c=c, pb=pb, j=j, s=s: e.transpose(out=pst[pb][:, j * 128:(j + 1) * 128],
                                                                             in_=hn32[:, c, s * 128:(s + 1) * 128], identity=C.ident[:]),
                             reads=d_hn32 + [C.d_ident], writes=[d_pst[pb]], inc=(j == 3))
                    if ev % 2 == 0:
                        P.op("dve", lambda e, pb=pb, ob=ob, cg=cg: e.tensor_copy(out=ostg[ob][:, cg * 512:(cg + 1) * 512], in_=pst[pb][:]),
                             reads=[d_pst[pb]], writes=[d_ostg[ob]])
                    else:
                        P.op("act", lambda e, pb=pb, ob=ob, cg=cg: e.copy(out=ostg[ob][:, cg * 512:(cg + 1) * 512], in_=pst[pb][:]),
                             reads=[d_pst[pb]], writes=[d_ostg[ob]])
                    ev += 1
                r0 = c0 + s * 128
                P.dma(A["out"][r0:r0 + 128, :], ostg[ob], reads=[d_ostg[ob]], final=True, queue="act")


def build_r(NT, KO, last):
    nc = new_nc()
    A = {}
    A["hT"] = din(nc, "hT", [KC, 128, NT], F32)
    A["oT"] = din(nc, "oT", [KO, 128, NT], BF16)
    A["pT"] = din(nc, "pT", [2, 128, NT], F32)
    A["g_mlp"] = din(nc, "g_mlp", [128, KC], F32)
    A["g_nxt"] = din(nc, "g_nxt", [128, KC], F32)
    A["wo"] = din(nc, "wo", [KO * 128 * D // (128 * 8192), 128, 8192], BF16)
    A["wup"] = din(nc, "wup", [16, 128, 8192], BF16)
    A["wdn"] = din(nc, "wdn", [16, 128, 8192], BF16)
    A["wg"] = din(nc, "wg", [4, 128, 8192], BF16)
    A["wpj"] = din(nc, "wpj", [128, 2 * D], BF16)
    ident = None
    if KO == 32:
        A["ss"] = din(nc, "ss", [2, NT], F32)
    if last:
        ident = din(nc, "ident", [128, 128], F32)
        A["out"] = dout(nc, "out", [NT, D], F32)
    else:
        A["hT_out"] = dout(nc, "hT_out", [KC, 128, NT], F32)
        A["hnT_out"] = dout(nc, "hnT_out", [KC, 128, NT], BF16)
    P = Prog(nc)
    C = Common(P, ident)
    emit_r(P, C, A, NT, KO, last)
    P.emit()
    return nc


S = 4096
NTT = S // TT


class Banks:
    def __init__(self, P, n, name="bk"):
        self.t = [P.ps(f"{name}{i}", [128, 512], F32) for i in range(n)]
        self.d = P.deps(n, name, excl=True)
        self.i = 0

    def next(self):
        i = self.i % len(self.t)
        self.i += 1
        return self.t[i], self.d[i]


def emit_m0_sb(P, C, A):
    bk = Banks(P, 6)
    wq = P.sb("wq", [128, KC, 512], BF16); d_wq = P.dep("wq")
    wk = P.sb("wk", [128, KC, 512], BF16); d_wk = P.dep("wk")
    wv = P.sb("wv", [128, KC, 512], BF16); d_wv = P.dep("wv")
    for i, (w, d) in enumerate([(wq, d_wq), (wk, d_wk), (wv, d_wv)]):
        P.dma(w[:], A["win"][i].rearrange("p (k c) -> p k c", c=512), writes=[d])
    tri = P.sb("tri", [128, 128], BF16); d_tri = P.dep("tri")
    P.dma(tri[:], A["tri"], writes=[d_tri])
    qT = P.sb("qT", [128, 4, S], BF16); d_qT = P.deps(NTT, "qT")
    kT = P.sb("kT", [128, 4, S], BF16); d_kT = P.deps(NTT, "kT")
    v = P.sb("v", [128, 32, 512], BF16); d_v = P.deps(NTT, "v")
    hn = [P.sb(f"hn{i}", [128, KC, TT], BF16) for i in range(1)]; d_hn = P.deps(1, "hn")
    ev = 0
    for tt in range(NTT):
        c0 = tt * TT
        P.dma(hn[0][:], A["hnT"][:, :, c0:c0 + TT].rearrange("c p t -> p c t"), writes=[d_hn[0]])
        for (w, d_w, dstT, d_dst, scale) in [(wq, d_wq, qT, d_qT, 128 ** -0.5), (wk, d_wk, kT, d_kT, 1.0)]:
            for hh in range(4):
                a, d_a = bk.next()
                for k in range(KC):
                    P.op("pe", lambda e, a=a, w=w, k=k, hh=hh: e.matmul(a[:], lhsT=w[:, k, hh * 128:(hh + 1) * 128], rhs=hn[0][:, k, :],
                                                                      start=(k == 0), stop=(k == KC - 1)),
                         reads=[d_w, d_hn[0]], writes=[d_a], inc=(k == KC - 1))
                if ev % 2 == 0:
                    P.op("act", lambda e, a=a, dstT=dstT, hh=hh, c0=c0, scale=scale: e.activation(out=dstT[:, hh, c0:c0 + TT], in_=a[:], func=AF.Copy, scale=scale),
                         reads=[d_a], writes=[d_dst[tt]])
                else:
                    P.op("dve", lambda e, a=a, dstT=dstT, hh=hh, c0=c0, scale=scale: e.tensor_scalar(out=dstT[:, hh, c0:c0 + TT], in0=a[:], scalar1=scale, scalar2=None, op0=ALU.mult),
                         reads=[d_a], writes=[d_dst[tt]])
                ev += 1
        for sub in range(4):
            a, d_a = bk.next()
            for k in range(KC):
                P.op("pe", lambda e, a=a, k=k, sub=sub: e.matmul(a[:], lhsT=hn[0][:, k, sub * 128:(sub + 1) * 128], rhs=wv[:, k, :],
                                                               start=(k == 0), stop=(k == KC - 1)),
                     reads=[d_wv, d_hn[0]], writes=[d_a], inc=(k == KC - 1))
            if ev % 2 == 0:
                P.op("act", lambda e, a=a, tt=tt, sub=sub: e.copy(out=v[:, tt * 4 + sub, :], in_=a[:]), reads=[d_a], writes=[d_v[tt]])
            else:
                P.op("dve", lambda e, a=a, tt=tt, sub=sub: e.tensor_copy(out=v[:, tt * 4 + sub, :], in_=a[:]), reads=[d_a], writes=[d_v[tt]])
            ev += 1
    import os
    if os.environ.get("SB_STAGE") == "inproj":
        for hh in range(4):
            P.dma(A["oT"][hh, :, 0:S], qT[:, hh, :], reads=d_qT, final=True)
            P.dma(A["oT"][4 + hh, :, 0:512], v[:, hh, :], reads=d_v, final=True)
        return
    print("SB attention starts at op", P.n_ops)
    NB = 3
    eb = [P.sb(f"e{i}", [128, 512], F32) for i in range(NB)]; d_e = P.deps(NB, "e")
    spb = [P.sb(f"sp{i}", [128, 512], BF16) for i in range(NB)]; d_sp = P.deps(NB, "sp")
    xb = [P.sb(f"x{i}", [128, 512], F32) for i in range(NB)]; d_x = P.deps(NB, "x")
    wb = [P.sb(f"w{i}", [128, 512], BF16) for i in range(NB)]; d_w = P.deps(NB, "w")
    cr = [P.sb(f"cr{i}", [1, 512], BF16) for i in range(2)]; d_cr = P.deps(2, "cr")
    osb = [P.sb(f"osb{i}", [128, 512], BF16) for i in range(2)]; d_osb = P.deps(2, "osb")
    Zb = [bk.t[0], bk.t[1]]; d_Z = [bk.d[0], bk.d[1]]
    Cb = [bk.t[2], bk.t[3]]; d_C = [bk.d[2], bk.d[3]]
    Ob = [bk.t[4], bk.t[5]]; d_O = [bk.d[4], bk.d[5]]
    tiles = []
    gi = 0
    for hh in range(4):
        for g in range(NTT):
            jmax = 4 * g + 3
            for j in range(jmax, -1, -1):
                tiles.append((hh, g, j, j == jmax, j == 0, gi))
            gi += 1

    def stA(i):
        hh, g, j, first, lastj, gi = tiles[i]
        z = Zb[i % 2]; dz = d_Z[i % 2]
        b = i % NB
        P.op("pe", lambda e: e.matmul(z[:], lhsT=kT[:, hh, j * 128:(j + 1) * 128], rhs=qT[:, hh, g * 512:(g + 1) * 512], start=True, stop=True),
             reads=[d_kT[j // 4], d_qT[g]], writes=[dz])
        P.op("act", lambda e: e.activation(out=eb[b][:], in_=z[:], func=AF.Exp), reads=[dz], writes=[d_e[b]])
        P.op("act", lambda e: e.activation(out=spb[b][:], in_=eb[b][:], func=AF.Ln, bias=1.0), reads=[d_e[b]], writes=[d_sp[b]])
        if j >= 4 * g:
            P.op("pool", lambda e: e.affine_select(out=spb[b][:], in_=spb[b][:], pattern=[[1, 512]], compare_op=ALU.is_gt, fill=0.0,
                                                   base=g * 512 - j * 128, channel_multiplier=-1),
                 reads=[d_sp[b]], writes=[d_sp[b]])

    def stB(i):
        hh, g, j, first, lastj, gi = tiles[i]
        c = Cb[i % 2]; dc = d_C[i % 2]
        b = i % NB
        P.op("pe", lambda e: e.matmul(c[:], lhsT=tri[:], rhs=spb[b][:], start=True, stop=first),
             reads=[d_tri, d_sp[b]], writes=[dc], inc=first)
        if not first:
            cprev = cr[(i - 1) % 2]
            P.op("pe", lambda e: e.matmul(c[:], lhsT=C.ones_bf[0:1, :], rhs=cprev[:], start=False, stop=True),
                 reads=[C.d_ones, d_cr[(i - 1) % 2]], writes=[dc])
        if not lastj:
            P.op("dve", lambda e: e.tensor_copy(out=cr[i % 2][:], in_=c[0:1, :]), reads=[dc], writes=[d_cr[i % 2]])
        P.op("act", lambda e: e.activation(out=xb[b][:], in_=c[:], func=AF.Exp, scale=-1.0), reads=[dc], writes=[d_x[b]])
        P.op("dve", lambda e: e.tensor_tensor(out=wb[b][:], in0=eb[b][:], in1=xb[b][:], op=ALU.mult),
             reads=[d_e[b], d_x[b]], writes=[d_w[b]])
        if j >= 4 * g:
            P.op("pool", lambda e: e.affine_select(out=wb[b][:], in_=wb[b][:], pattern=[[1, 512]], compare_op=ALU.is_gt, fill=0.0,
                                                   base=g * 512 - j * 128, channel_multiplier=-1),
                 reads=[d_w[b]], writes=[d_w[b]])

    def stC(i):
        hh, g, j, first, lastj, gi = tiles[i]
        o = Ob[gi % 2]; do = d_O[gi % 2]
        b = i % NB
        P.op("pe", lambda e: e.matmul(o[:], lhsT=v[:, j, hh * 128:(hh + 1) * 128], rhs=wb[b][:], start=first, stop=lastj),
             reads=[d_v[j // 4], d_w[b]], writes=[do], inc=lastj)
        if lastj:
            ob = gi % 2
            P.op("dve", lambda e: e.tensor_copy(out=osb[ob][:], in_=o[:]), reads=[do], writes=[d_osb[ob]])
            P.dma(A["oT"][hh, :, g * 512:(g + 1) * 512], osb[ob][:], reads=[d_osb[ob]], final=True, queue="act")

    n = len(tiles)
    if os.environ.get("SB_NT"):
        n = int(os.environ["SB_NT"])
    for i in range(n + 2):
        if i < n:
            stA(i)
        if 0 <= i - 1 < n:
            stB(i - 1)
        if 0 <= i - 2 < n:
            stC(i - 2)


def emit_m0_gla(P, C, A):
    bk = Banks(P, 4)
    Ob = [P.ps(f"gO{i}", [128, 512], F32) for i in range(4)]; d_O = P.deps(4, "gO", excl=True)
    wqk = P.sb("wqk", [128, KC, 512], BF16); d_wqk = P.dep("wqk")
    wgv = P.sb("wgv", [128, KC, 512], BF16); d_wgv = P.dep("wgv")
    wgo = P.sb("wgo", [128, KC, 512], BF16); d_wgo = P.dep("wgo")
    for i, (w, d) in enumerate([(wqk, d_wqk), (wgv, d_wgv), (wgo, d_wgo)]):
        P.dma(w[:], A["win"][3 + i].rearrange("p (k c) -> p k c", c=512), writes=[d])
    wlr = P.sb("wlr", [128, KC, 16], BF16); d_wlr = P.dep("wlr")
    P.dma(wlr[:], A["wlr"].rearrange("p (k c) -> p k c", c=16), writes=[d_wlr])
    wgu = P.sb("wgu", [16, 256], BF16); d_wgu = P.dep("wgu")
    P.dma(wgu[:], A["wgu"], writes=[d_wgu])
    negb = P.sb("negb", [128, 2], F32); d_negb = P.dep("negb")
    P.dma(negb[:], A["bgate"], writes=[d_negb])
    P.op("dve", lambda e: e.tensor_scalar(out=negb[:], in0=negb[:], scalar1=-1.0, scalar2=None, op0=ALU.mult), reads=[d_negb], writes=[d_negb])
    gnorm = P.sb("gnorm", [128, 1], F32); d_gnorm = P.dep("gnorm")
    P.dma(gnorm[:], A["gnorm"], writes=[d_gnorm])
    m2 = P.sb("m2", [128, 128], F32); d_m2 = P.dep("m2")
    P.dma(m2[:], A["m2"], writes=[d_m2])
    rmask = P.sb("rmask", [128, TT], F32); d_rmask = P.dep("rmask")
    P.dma(rmask[:], A["rmask"], writes=[d_rmask])
    identb = P.sb("identb", [128, 128], BF16); d_identb = P.dep("identb")
    P.dma(identb[:], A["identb"], writes=[d_identb])
    hn = [P.sb(f"ghn{i}", [128, KC, TT], BF16) for i in range(2)]; d_hn = P.deps(2, "ghn")
    gq = P.sb("gq", [128, 2, TT], BF16); d_gq = P.dep("gq")
    gk = P.sb("gk", [128, 2, TT], BF16); d_gk = P.dep("gk")
    gv = P.sb("gv", [128, 4, 512], BF16); d_gv = P.dep("gv")
    gs = P.sb("gs", [128, 4, TT], BF16); d_gs = P.dep("gs")
    sg = [P.sb(f"sg{i}", [128, TT], F32) for i in range(2)]; d_sg = P.deps(2, "sg")
    glr = P.sb("glr", [16, TT], BF16); d_glr = P.dep("glr")
    ee = P.sb("ee", [128, TT], F32); d_ee = P.dep("ee")
    la = P.sb("la", [128, 2, TT], F32); d_la = P.dep("la")
    E1 = P.sb("E1", [128, 2, TT], F32); d_E1 = P.dep("E1")
    E2 = P.sb("E2", [128, 2, TT], F32); d_E2 = P.dep("E2")
    qd = P.sb("qd", [128, 2, TT], BF16); d_qd = P.dep("qd")
    ki = P.sb("ki", [128, 2, TT], BF16); d_ki = P.dep("ki")
    kt = P.sb("kt", [128, 4, 2, 128], BF16); d_kt = P.deps(8, "kt")
    sm = [P.sb(f"sm{i}", [128, 128], BF16) for i in range(4)]; d_sm = P.deps(4, "sm")
    St = P.sb("St", [128, 2, 128], F32); d_St = P.deps(2, "St")
    Stmp = P.sb("Stmp", [128, 2, 128], F32); d_Stmp = P.deps(2, "Stmp")
    Sbf = [P.sb(f"Sbf{i}", [128, 2, 128], BF16) for i in range(2)]; d_Sbf = [P.deps(2, f"Sbf{i}") for i in range(2)]
    sq = P.sb("gsq", [128, TT], BF16); d_sq = P.dep("gsq")
    rr = P.sb("grr", [128, TT], F32); d_rr = P.dep("grr")
    on = P.sb("gon", [128, TT], F32); d_on = P.dep("gon")
    og = [P.sb(f"gog{i}", [128, TT], BF16) for i in range(2)]; d_og = P.deps(2, "gog")
    P.op("pool", lambda e: e.memset(St[:], 0.0), writes=d_St)
    for hp in range(2):
        P.op("pool", lambda e, hp=hp: e.memset(Sbf[0][:, hp, :], 0.0), writes=[d_Sbf[0][hp]])
    sidx = [0, 0]
    ev = 0
    smi = 0
    for tt in range(NTT):
        c0 = tt * TT
        hb = hn[tt % 2]; dhb = d_hn[tt % 2]
        P.dma(hb[:], A["hnT"][:, :, c0:c0 + TT].rearrange("c p t -> p c t"), writes=[dhb])
        for cc in range(4):
            a, d_a = bk.next()
            for k in range(KC):
                P.op("pe", lambda e, a=a, k=k, cc=cc, hb=hb: e.matmul(a[:], lhsT=wqk[:, k, cc * 128:(cc + 1) * 128], rhs=hb[:, k, :],
                                                             start=(k == 0), stop=(k == KC - 1)),
                     reads=[d_wqk, dhb], writes=[d_a], inc=(k == KC - 1))
            dst, d_dst, sc = (gq, d_gq, 0.125) if cc < 2 else (gk, d_gk, 1.0)
            P.op("act", lambda e, a=a, dst=dst, cc=cc, sc=sc: e.activation(out=dst[:, cc % 2, :], in_=a[:], func=AF.Copy, scale=sc),
                 reads=[d_a], writes=[d_dst])
        for sub in range(4):
            a, d_a = bk.next()
            for k in range(KC):
                P.op("pe", lambda e, a=a, k=k, sub=sub, hb=hb: e.matmul(a[:], lhsT=hb[:, k, sub * 128:(sub + 1) * 128], rhs=wgv[:, k, :],
                                                               start=(k == 0), stop=(k == KC - 1)),
                     reads=[d_wgv, dhb], writes=[d_a], inc=(k == KC - 1))
            P.op("dve", lambda e, a=a, sub=sub: e.tensor_copy(out=gv[:, sub, :], in_=a[:]), reads=[d_a], writes=[d_gv])
        for hh in range(4):
            a, d_a = bk.next()
            for k in range(KC):
                P.op("pe", lambda e, a=a, k=k, hh=hh, hb=hb: e.matmul(a[:], lhsT=wgo[:, k, hh * 128:(hh + 1) * 128], rhs=hb[:, k, :],
                                                             start=(k == 0), stop=(k == KC - 1)),
                     reads=[d_wgo, dhb], writes=[d_a], inc=(k == KC - 1))
            sb_ = hh % 2
            P.op("act", lambda e, a=a, sb_=sb_: e.activation(out=sg[sb_][:], in_=a[:], func=AF.Sigmoid), reads=[d_a], writes=[d_sg[sb_]])
            P.op("dve", lambda e, a=a, sb_=sb_, hh=hh: e.tensor_tensor(out=gs[:, hh, :], in0=a[:], in1=sg[sb_][:], op=ALU.mult),
                 reads=[d_a, d_sg[sb_]], writes=[d_gs])
        a, d_a = bk.next()
        for k in range(KC):
            P.op("pe", lambda e, a=a, k=k, hb=hb: e.matmul(a[0:16, :], lhsT=wlr[:, k, :], rhs=hb[:, k, :], start=(k == 0), stop=(k == KC - 1)),
                 reads=[d_wlr, dhb], writes=[d_a], inc=(k == KC - 1))
        P.op("act", lambda e, a=a: e.copy(out=glr[:], in_=a[0:16, :]), reads=[d_a], writes=[d_glr])
        for cc in range(2):
            a, d_a = bk.next()
            P.op("pe", lambda e, a=a, cc=cc: e.matmul(a[:], lhsT=wgu[:, cc * 128:(cc + 1) * 128], rhs=glr[:], start=True, stop=True),
                 reads=[d_wgu, d_glr], writes=[d_a])
            P.op("act", lambda e, a=a, cc=cc: e.activation(out=ee[:], in_=a[:], func=AF.Exp, bias=negb[:, cc:cc + 1], scale=-1.0),
                 reads=[d_a, d_negb], writes=[d_ee])
            P.op("act", lambda e, cc=cc: e.activation(out=la[:, cc, :], in_=ee[:], func=AF.Ln, bias=1.0), reads=[d_ee], writes=[d_la])
            P.op("dve", lambda e, cc=cc: e.tensor_tensor_scan(out=la[:, cc, :], data0=rmask[:], data1=la[:, cc, :], initial=0.0,
                                                              op0=ALU.mult, op1=ALU.add),
                 reads=[d_la, d_rmask], writes=[d_la])
        P.op("act", lambda e: e.activation(out=E1[:], in_=la[:], func=AF.Exp, scale=-1.0 / 16), reads=[d_la], writes=[d_E1])
        P.op("act", lambda e: e.activation(out=E2[:], in_=la[:], func=AF.Exp, scale=1.0 / 16), reads=[d_la], writes=[d_E2])
        P.op("dve", lambda e: e.tensor_tensor(out=qd[:], in0=gq[:], in1=E1[:], op=ALU.mult), reads=[d_gq, d_E1], writes=[d_qd])
        P.op("pool", lambda e: e.tensor_tensor(out=ki[:], in0=gk[:], in1=E2[:], op=ALU.mult), reads=[d_gk, d_E2], writes=[d_ki])
        for s in range(4):
            t0 = s * 128
            for hp in range(2):
                a, d_a = bk.next()
                av = a[:].bitcast(BF16)[:, 0:128]
                P.op("pe", lambda e, av=av, hp=hp, t0=t0: e.transpose(out=av, in_=ki[:, hp, t0:t0 + 128], identity=identb[:]),
                     reads=[d_ki, d_identb], writes=[d_a])
                P.op("act", lambda e, av=av, s=s, hp=hp: e.copy(out=kt[:, s, hp, :], in_=av), reads=[d_a], writes=[d_kt[s * 2 + hp]])
                for h2 in range(2):
                    hh = hp * 2 + h2
                    r0 = h2 * 64
                    a, d_a = bk.next()
                    P.op("pe", lambda e, a=a, r0=r0, hp=hp, t0=t0: e.matmul(a[:, 0:128], lhsT=ki[r0:r0 + 64, hp, t0:t0 + 128], rhs=qd[r0:r0 + 64, hp, t0:t0 + 128],
                                                                           start=True, stop=True),
                         reads=[d_ki, d_qd], writes=[d_a])
                    si = smi % 4
                    smi += 1
                    P.op("dve", lambda e, a=a, si=si: e.tensor_tensor(out=sm[si][:], in0=a[:, 0:128], in1=m2[:], op=ALU.mult),
                         reads=[d_a, d_m2], writes=[d_sm[si]])
                    P.op("pe", lambda e, hh=hh, s=s, si=si, t0=t0: e.matmul(Ob[hh][:, t0:t0 + 128], lhsT=gv[:, s, hh * 128:(hh + 1) * 128], rhs=sm[si][:],
                                                                           start=True, stop=False),
                         reads=[d_gv, d_sm[si]], writes=[d_O[hh]], inc=False)
                for cc in range(2):
                    cur = sidx[hp]
                    q0 = t0 + cc * 64
                    for h2 in range(2):
                        hh = hp * 2 + h2
                        r0 = h2 * 64
                        P.op("pe", lambda e, hh=hh, r0=r0, hp=hp, q0=q0, cur=cur, cc=cc: e.matmul(Ob[hh][:, q0:q0 + 64], lhsT=Sbf[cur][r0:r0 + 64, hp, :], rhs=qd[r0:r0 + 64, hp, q0:q0 + 64],
                                                                                                   start=False, stop=(cc == 1)),
                             reads=[d_Sbf[cur][hp], d_qd], writes=[d_O[hh]], inc=(cc == 1))
                    a, d_a = bk.next()
                    P.op("pe", lambda e, a=a, cc=cc, s=s, hp=hp: e.matmul(a[:, 0:256], lhsT=kt[cc * 64:(cc + 1) * 64, s, hp, :], rhs=gv[cc * 64:(cc + 1) * 64, s, hp * 256:(hp + 1) * 256],
                                                                         start=True, stop=True),
                         reads=[d_kt[s * 2 + hp], d_gv], writes=[d_a])
                    nxt = 1 - cur
                    dcol = q0 + 63
                    for h2 in range(2):
                        r0 = h2 * 64
                        P.op("dve", lambda e, a=a, r0=r0, hp=hp, h2=h2: e.tensor_tensor(out=Stmp[r0:r0 + 64, hp, :], in0=a[r0:r0 + 64, h2 * 128:(h2 + 1) * 128], in1=St[r0:r0 + 64, hp, :], op=ALU.add),
                             reads=[d_a, d_St[hp]], writes=[d_Stmp[hp]])
                        P.op("dve", lambda e, r0=r0, hp=hp, dcol=dcol: e.tensor_scalar(out=St[r0:r0 + 64, hp, :], in0=Stmp[r0:r0 + 64, hp, :], scalar1=E1[r0:r0 + 64, hp, dcol:dcol + 1], scalar2=None, op0=ALU.mult),
                             reads=[d_Stmp[hp], d_E1], writes=[d_St[hp]])
                    P.op("pool", lambda e, hp=hp, nxt=nxt: e.tensor_copy(out=Sbf[nxt][:, hp, :], in_=St[:, hp, :]), reads=[d_St[hp]], writes=[d_Sbf[nxt][hp]])
                    sidx[hp] = nxt
        for hh in range(4):
            P.op("act", lambda e, hh=hh: e.activation(out=sq[:], in_=Ob[hh][:], func=AF.Square), reads=[d_O[hh]], writes=[d_sq])
            a, d_a = bk.next()
            P.op("pe", lambda e, a=a: e.matmul(a[:], lhsT=C.ones_bf[:], rhs=sq[:], start=True, stop=True), reads=[C.d_ones, d_sq], writes=[d_a])
            P.op("act", lambda e, a=a: e.activation(out=rr[:], in_=a[:], func=AF.Sqrt, bias=C.eps[:], scale=1.0 / 128), reads=[d_a, C.d_eps], writes=[d_rr])
            P.op("dve", lambda e: e.reciprocal(out=rr[:], in_=rr[:]), reads=[d_rr], writes=[d_rr])
            P.op("dve", lambda e, hh=hh: e.scalar_tensor_tensor(out=on[:], in0=Ob[hh][:], scalar=gnorm[:, 0:1], in1=rr[:], op0=ALU.mult, op1=ALU.mult),
                 reads=[d_O[hh], d_gnorm, d_rr], writes=[d_on])
            ob = hh % 2
            P.op("pool", lambda e, hh=hh, ob=ob: e.tensor_tensor(out=og[ob][:], in0=on[:], in1=gs[:, hh, :], op=ALU.mult),
                 reads=[d_on, d_gs], writes=[d_og[ob]])
            P.dma(A["oT"][4 + hh, :, c0:c0 + TT], og[ob][:], reads=[d_og[ob]], final=True, queue="act")


def build_m0(parts=("sb", "gla")):
    nc = new_nc()
    A = {}
    A["hnT"] = din(nc, "hnT", [KC, 128, S], BF16)
    A["win"] = din(nc, "win", [6, 128, 8192], BF16)
    A["wlr"] = din(nc, "wlr", [128, KC * 16], BF16)
    A["wgu"] = din(nc, "wgu", [16, 256], BF16)
    A["bgate"] = din(nc, "bgate", [128, 2], F32)
    A["gnorm"] = din(nc, "gnorm", [128, 1], F32)
    A["tri"] = din(nc, "tri", [128, 128], BF16)
    A["m2"] = din(nc, "m2", [128, 128], F32)
    A["rmask"] = din(nc, "rmask", [128, TT], F32)
    A["identb"] = din(nc, "identb", [128, 128], BF16)
    A["oT"] = dout(nc, "oT", [8, 128, S], BF16)
    P = Prog(nc)
    C = Common(P)
    if "sb" in parts:
        with P.scope():
            emit_m0_sb(P, C, A)
    if "gla" in parts:
        with P.scope():
            emit_m0_gla(P, C, A)
    P.emit()
    return nc


def emit_m1(P, C, A):
    bk = Banks(P, 8)
    ring = WRing(P, 2, "m1w")
    def cload(name, shape, dt, src):
        t = P.sb(name, shape, dt); d = P.dep(name)
        P.dma(t[:], src, writes=[d])
        return t, d
    wdt, d_wdt = cload("wdt", [128, KC, 32], BF16, A["wdt"].rearrange("p (k c) -> p k c", c=32))
    cw, d_cw = cload("cw", [128, 24, 4], F32, A["convw"].rearrange("p (c i) -> p c i", i=4))
    cb_, d_cb = cload("cb", [128, 24], F32, A["convb"])
    dtb, d_dtb = cload("dtb", [128, 32], F32, A["dtbias"])
    arow, d_arow = cload("arow", [128, 32], F32, A["alog"])
    dsk, d_dsk = cload("dsk", [128, 2048], F32, A["dskip"])
    gno, d_gno = cload("gno", [128, KC], F32, A["gnorm"])
    triI, d_triI = cload("triI", [128, 128], F32, A["triI"])
    triS, d_triS = cload("triS", [128, 128], F32, A["triS"])
    ones32, d_ones32 = cload("ones32", [128, 128], F32, A["ones32"])
    identb, d_identb = cload("identb", [128, 128], BF16, A["identb"])
    P.op("act", lambda e: e.activation(out=arow[:], in_=arow[:], func=AF.Exp), reads=[d_arow], writes=[d_arow])
    P.op("dve", lambda e: e.tensor_scalar(out=arow[:], in0=arow[:], scalar1=-1.0, scalar2=None, op0=ALU.mult), reads=[d_arow], writes=[d_arow])
    hn = P.sb("hn", [128, KC, TT], BF16); d_hn = P.dep("hn")
    xsT = P.sb("xsT", [128, KC, TT], BF16); d_xsT = P.deps(KC, "xsT")
    BT = P.sb("BT", [128, 4, TT], BF16); d_BT = P.deps(4, "BT")
    CT = P.sb("CT", [128, 4, TT], BF16); d_CT = P.deps(4, "CT")
    raw = [P.sb(f"raw{i}", [128, TT + 3], F32) for i in range(2)]; d_raw = P.deps(2, "raw")
    cacc = [P.sb(f"cacc{i}", [128, TT], F32) for i in range(2)]; d_cacc = P.deps(2, "cacc")
    sig = [P.sb(f"sig{i}", [128, TT], F32) for i in range(2)]; d_sig = P.deps(2, "sig")
    hist = P.sb("hist", [128, 24, 3], F32); d_hist = P.deps(24, "hist")
    P.op("pool", lambda e: e.memset(hist[:], 0.0), writes=d_hist)
    zs = P.sb("zs", [128, 4, 2048], BF16); d_zs = P.deps(4, "zs")
    xtok = [P.sb(f"xtok{i}", [128, 2048], BF16) for i in range(2)]; d_xtok = P.deps(2, "xtok")
    xdt = [P.sb(f"xdt{i}", [128, 2048], BF16) for i in range(2)]; d_xdt = P.deps(2, "xdt")
    xdtd = [P.sb(f"xdtd{i}", [128, 2048], BF16) for i in range(2)]; d_xdtd = P.deps(2, "xdtd")
    Btok = P.sb("Btok", [128, 4, 512], BF16); d_Btok = P.deps(4, "Btok")
    dtt = P.sb("dtt", [128, 4, 32], F32); d_dtt = P.deps(4, "dtt")
    da = P.sb("da", [128, 4, 32], F32); d_da = P.deps(4, "da")
    ex3 = P.sb("ex3", [128, 4, 96], F32); d_ex3 = P.deps(4, "ex3")
    cbm = P.sb("cbm", [128, 128], F32); d_cbm = P.dep("cbm")
    Bm = P.sb("Bm", [128, 8, 128], F32); d_Bm = P.dep("Bm")
    Eb = P.sb("Eb", [128, 8, 128], F32); d_Eb = P.dep("Eb")
    G = [P.sb(f"G{i}", [128, 8, 128], BF16) for i in range(2)]; d_G = P.deps(2, "G")
    prev = P.sb("prev", [128, 4, 512], F32); d_prev = P.deps(4, "prev")
    prevb = P.sb("prevb", [128, 4, 512], BF16); d_prevb = P.deps(4, "prevb")
    P.op("pool", lambda e: e.memset(prev[:], 0.0), writes=d_prev)
    P.op("pool", lambda e: e.memset(prevb[:], 0.0), writes=d_prevb)
    yt = [P.sb(f"yt{i}", [128, 512], F32) for i in range(4)]; d_yt = P.deps(4, "yt")
    yzb = P.sb("yzb", [128, 2048], BF16); d_yzb = P.deps(4, "yzb")
    ssq = P.sb("ssq", [128, 4, 4], F32); d_ssq = P.deps(4, "ssq")
    sst = P.sb("sst", [128, 4], F32); d_sst = P.dep("sst")
    yT = P.sb("yT", [128, KC, TT], BF16); d_yT = P.dep("yT")
    jk = P.sb("jk", [128, 512], F32); d_jk = P.dep("jk")
    evn = [0]

    def conv_chunk(ps, d_ps, ch, dst, d_dst):
        i = evn[0] % 2
        evn[0] += 1
        r, dr = raw[i], d_raw[i]
        P.op("pool", lambda e: e.tensor_copy(out=r[:, 0:3], in_=hist[:, ch, :]), reads=[d_hist[ch]], writes=[dr])
        P.op("act", lambda e: e.copy(out=r[:, 3:TT + 3], in_=ps[:]), reads=[d_ps], writes=[dr])
        P.op("pool", lambda e: e.tensor_copy(out=hist[:, ch, :], in_=r[:, TT:TT + 3]), reads=[dr], writes=[d_hist[ch]])
        ca, dca = cacc[i], d_cacc[i]
        P.op("dve", lambda e: e.tensor_scalar(out=ca[:], in0=r[:, 3:TT + 3], scalar1=cw[:, ch, 3:4], scalar2=cb_[:, ch:ch + 1], op0=ALU.mult, op1=ALU.add),
             reads=[dr, d_cw, d_cb], writes=[dca])
        for q in range(1, 4):
            P.op("dve", lambda e, q=q: e.scalar_tensor_tensor(out=ca[:], in0=r[:, 3 - q:TT + 3 - q], scalar=cw[:, ch, 3 - q:4 - q], in1=ca[:], op0=ALU.mult, op1=ALU.add),
                 reads=[dr, d_cw, dca], writes=[dca])
        sg, dsg = sig[i], d_sig[i]
        P.op("act", lambda e: e.activation(out=sg[:], in_=ca[:], func=AF.Sigmoid), reads=[dca], writes=[dsg])
        P.op("pool", lambda e: e.tensor_tensor(out=dst, in0=ca[:], in1=sg[:], op=ALU.mult), reads=[dca, dsg], writes=[d_dst])

    for tt in range(NTT):
        c0 = tt * TT
        P.dma(hn[:], A["hnT"][:, :, c0:c0 + TT].rearrange("c p t -> p c t"), writes=[d_hn])
        for sub in range(4):
            a, d_a = bk.next()
            for k in range(KC):
                P.op("pe", lambda e, a=a, k=k, sub=sub: e.matmul(a[:, 0:32], lhsT=hn[:, k, sub * 128:(sub + 1) * 128], rhs=wdt[:, k, :], start=(k == 0), stop=(k == KC - 1)),
                     reads=[d_hn, d_wdt], writes=[d_a], inc=(k == KC - 1))
            P.op("dve", lambda e, a=a, sub=sub: e.tensor_tensor(out=dtt[:, sub, :], in0=a[:, 0:32], in1=dtb[:], op=ALU.add), reads=[d_a, d_dtb], writes=[d_dtt[sub]])
            P.op("act", lambda e, sub=sub: e.activation(out=dtt[:, sub, :], in_=dtt[:, sub, :], func=AF.Exp), reads=[d_dtt[sub]], writes=[d_dtt[sub]])
            P.op("act", lambda e, sub=sub: e.activation(out=dtt[:, sub, :], in_=dtt[:, sub, :], func=AF.Ln, bias=1.0), reads=[d_dtt[sub]], writes=[d_dtt[sub]])
            P.op("dve", lambda e, sub=sub: e.tensor_tensor(out=da[:, sub, :], in0=dtt[:, sub, :], in1=arow[:], op=ALU.mult), reads=[d_dtt[sub], d_arow], writes=[d_da[sub]])
            a, d_a = bk.next()
            for i3, (m, dm) in enumerate([(triI, d_triI), (triS, d_triS), (ones32, d_ones32)]):
                P.op("pe", lambda e, a=a, m=m, i3=i3, sub=sub: e.matmul(a[:, i3 * 32:(i3 + 1) * 32], lhsT=m[:], rhs=da[:, sub, :], start=True, stop=True),
                     reads=[dm, d_da[sub]], writes=[d_a], inc=(i3 == 2))
            P.op("act", lambda e, a=a, sub=sub: e.activation(out=ex3[:, sub, :], in_=a[:, 0:96], func=AF.Exp), reads=[d_a], writes=[d_ex3[sub]])
        for which in range(6):
            pan, d_pan = ring.load(A["win"][4 + which])
            panv = pan[:].rearrange("p (k c) -> p k c", c=512)
            for j in range(4):
                a, d_a = bk.next()
                for k in range(KC):
                    P.op("pe", lambda e, a=a, panv=panv, k=k, j=j: e.matmul(a[:], lhsT=panv[:, k, j * 128:(j + 1) * 128], rhs=hn[:, k, :], start=(k == 0), stop=(k == KC - 1)),
                         reads=[d_pan, d_hn], writes=[d_a], inc=(k == KC - 1))
                if which < 4:
                    ch = which * 4 + j
                    conv_chunk(a, d_a, ch, xsT[:, ch, :], d_xsT[ch])
                elif which == 4:
                    conv_chunk(a, d_a, 16 + j, BT[:, j, :], d_BT[j])
                else:
                    conv_chunk(a, d_a, 20 + j, CT[:, j, :], d_CT[j])
        for pz in range(4):
            pan, d_pan = ring.load(A["win"][pz])
            panv = pan[:].rearrange("p (k c) -> p k c", c=512)
            for sub in range(4):
                a, d_a = bk.next()
                for k in range(KC):
                    P.op("pe", lambda e, a=a, panv=panv, k=k, sub=sub: e.matmul(a[:], lhsT=hn[:, k, sub * 128:(sub + 1) * 128], rhs=panv[:, k, :], start=(k == 0), stop=(k == KC - 1)),
                         reads=[d_pan, d_hn], writes=[d_a], inc=(k == KC - 1))
                i = evn[0] % 2
                evn[0] += 1
                P.op("act", lambda e, a=a, i=i: e.activation(out=sig[i][:], in_=a[:], func=AF.Sigmoid), reads=[d_a], writes=[d_sig[i]])
                P.op("dve", lambda e, a=a, i=i, sub=sub, pz=pz: e.tensor_tensor(out=zs[:, sub, pz * 512:(pz + 1) * 512], in0=a[:], in1=sig[i][:], op=ALU.mult),
                     reads=[d_a, d_sig[i]], writes=[d_zs[sub]])
        for sub in range(4):
            t0 = sub * 128
            xb = sub % 2
            for g in range(4):
                a, d_a = bk.next()
                av = a[:].bitcast(BF16)[:, 0:512]
                for j in range(4):
                    ch = g * 4 + j
                    P.op("pe", lambda e, av=av, ch=ch, j=j, t0=t0: e.transpose(out=av[:, j * 128:(j + 1) * 128], in_=xsT[:, ch, t0:t0 + 128], identity=identb[:]),
                         reads=[d_xsT[ch], d_identb], writes=[d_a], inc=(j == 3))
                P.op("act", lambda e, av=av, g=g, xb=xb: e.copy(out=xtok[xb][:, g * 512:(g + 1) * 512], in_=av), reads=[d_a], writes=[d_xtok[xb]])
                P.op("dve", lambda e, av=av, g=g, xb=xb, sub=sub: e.tensor_tensor(out=xdt[xb][:, g * 512:(g + 1) * 512].rearrange("p (h d) -> p h d", d=64),
                                                                               in0=av.rearrange("p (h d) -> p h d", d=64),
                                                                               in1=dtt[:, sub, g * 8:(g + 1) * 8].unsqueeze(2).to_broadcast([128, 8, 64]), op=ALU.mult),
                     reads=[d_a, d_dtt[sub]], writes=[d_xdt[xb]])
                P.op("pool", lambda e, g=g, xb=xb, sub=sub: e.tensor_tensor(out=xdtd[xb][:, g * 512:(g + 1) * 512].rearrange("p (h d) -> p h d", d=64),
                                                                          in0=xdt[xb][:, g * 512:(g + 1) * 512].rearrange("p (h d) -> p h d", d=64),
                                                                          in1=ex3[:, sub, 32 + g * 8:32 + (g + 1) * 8].unsqueeze(2).to_broadcast([128, 8, 64]), op=ALU.mult),
                     reads=[d_xdt[xb], d_ex3[sub]], writes=[d_xdtd[xb]])
            a, d_a = bk.next()
            av = a[:].bitcast(BF16)[:, 0:512]
            for g in range(4):
                P.op("pe", lambda e, av=av, g=g, t0=t0: e.transpose(out=av[:, g * 128:(g + 1) * 128], in_=BT[:, g, t0:t0 + 128], identity=identb[:]),
                     reads=[d_BT[g], d_identb], writes=[d_a], inc=(g == 3))
            P.op("act", lambda e, av=av, sub=sub: e.copy(out=Btok[:, sub, :], in_=av), reads=[d_a], writes=[d_Btok[sub]])
            for g in range(4):
                gi = (tt * 4 + sub) * 4 + g
                a, d_a = bk.next()
                P.op("pe", lambda e, a=a, g=g, t0=t0: e.matmul(a[:, 0:128], lhsT=BT[:, g, t0:t0 + 128], rhs=CT[:, g, t0:t0 + 128], start=True, stop=True),
                     reads=[d_BT[g], d_CT[g]], writes=[d_a])
                P.op("dve", lambda e, a=a: e.tensor_tensor(out=cbm[:], in0=a[:, 0:128], in1=triI[:], op=ALU.mult), reads=[d_a, d_triI], writes=[d_cbm])
                P.op("pool", lambda e, g=g, sub=sub: e.tensor_tensor(out=Bm[:], in0=triI[:].unsqueeze(1).to_broadcast([128, 8, 128]),
                                                                    in1=da[:, sub, g * 8:(g + 1) * 8].unsqueeze(2).to_broadcast([128, 8, 128]), op=ALU.mult),
                     reads=[d_triI, d_da[sub]], writes=[d_Bm])
                s1, d_s1 = bk.next()
                s2, d_s2 = bk.next()
                Bmf = Bm[:].rearrange("p h l -> p (h l)")
                P.op("pe", lambda e, s1=s1, Bmf=Bmf: e.matmul(s1[:], lhsT=triS[:], rhs=Bmf[:, 0:512], start=True, stop=True), reads=[d_triS, d_Bm], writes=[d_s1])
                P.op("pe", lambda e, s2=s2, Bmf=Bmf: e.matmul(s2[:], lhsT=triS[:], rhs=Bmf[:, 512:1024], start=True, stop=True), reads=[d_triS, d_Bm], writes=[d_s2])
                Ebf = Eb[:].rearrange("p h l -> p (h l)")
                P.op("act", lambda e, s1=s1, Ebf=Ebf: e.activation(out=Ebf[:, 0:512], in_=s1[:], func=AF.Exp), reads=[d_s1], writes=[d_Eb])
                P.op("act", lambda e, s2=s2, Ebf=Ebf: e.activation(out=Ebf[:, 512:1024], in_=s2[:], func=AF.Exp), reads=[d_s2], writes=[d_Eb])
                Gb, d_Gb = G[gi % 2], d_G[gi % 2]
                P.op("dve", lambda e, Gb=Gb: e.tensor_tensor(out=Gb[:], in0=Eb[:], in1=cbm[:].unsqueeze(1).to_broadcast([128, 8, 128]), op=ALU.mult),
                     reads=[d_Eb, d_cbm], writes=[d_Gb])
                yd, d_yd = bk.next()
                for hh in range(8):
                    P.op("pe", lambda e, yd=yd, Gb=Gb, hh=hh, g=g, xb=xb: e.matmul(yd[:, hh * 64:(hh + 1) * 64], lhsT=Gb[:, hh, :], rhs=xdt[xb][:, g * 512 + hh * 64:g * 512 + (hh + 1) * 64], start=True, stop=True),
                         reads=[d_Gb, d_xdt[xb]], writes=[d_yd], inc=(hh == 7))
                yo, d_yo = bk.next()
                P.op("pe", lambda e, yo=yo, g=g, t0=t0: e.matmul(yo[:], lhsT=CT[:, g, t0:t0 + 128], rhs=prevb[:, g, :], start=True, stop=True),
                     reads=[d_CT[g], d_prevb[g]], writes=[d_yo])
                st, d_st = bk.next()
                P.op("pe", lambda e, st=st, g=g, sub=sub, xb=xb: e.matmul(st[:], lhsT=Btok[:, sub, g * 128:(g + 1) * 128], rhs=xdtd[xb][:, g * 512:(g + 1) * 512], start=True, stop=True),
                     reads=[d_Btok[sub], d_xdtd[xb]], writes=[d_st])
                y0, y1 = yt[(gi * 2) % 4], yt[(gi * 2 + 1) % 4]
                dy0, dy1 = d_yt[(gi * 2) % 4], d_yt[(gi * 2 + 1) % 4]
                P.op("dve", lambda e, yo=yo, y0=y0, g=g, sub=sub: e.tensor_tensor(out=y0[:].rearrange("p (h d) -> p h d", d=64), in0=yo[:].rearrange("p (h d) -> p h d", d=64),
                                                                                  in1=ex3[:, sub, g * 8:(g + 1) * 8].unsqueeze(2).to_broadcast([128, 8, 64]), op=ALU.mult),
                     reads=[d_yo, d_ex3[sub]], writes=[dy0])
                P.op("dve", lambda e, yd=yd, y0=y0: e.tensor_tensor(out=y0[:], in0=yd[:], in1=y0[:], op=ALU.add), reads=[d_yd, dy0], writes=[dy0])
                P.op("pool", lambda e, y1=y1, g=g, xb=xb: e.tensor_tensor(out=y1[:], in0=xtok[xb][:, g * 512:(g + 1) * 512], in1=dsk[:, g * 512:(g + 1) * 512], op=ALU.mult),
                     reads=[d_xtok[xb], d_dsk], writes=[dy1])
                P.op("pool", lambda e, y0=y0, y1=y1: e.tensor_tensor(out=y1[:], in0=y1[:], in1=y0[:], op=ALU.add), reads=[dy0, dy1], writes=[dy1])
                P.op("pool", lambda e, y1=y1, g=g, sub=sub: e.tensor_tensor(out=y1[:], in0=y1[:], in1=zs[:, sub, g * 512:(g + 1) * 512], op=ALU.mult), reads=[dy1, d_zs[sub]], writes=[dy1])
                P.op("act", lambda e, y1=y1, g=g, sub=sub: e.activation(out=jk[:], in_=y1[:], func=AF.Square, accum_out=ssq[:, sub, g:g + 1]),
                     reads=[dy1], writes=[d_jk, d_ssq[sub]])
                P.op("act", lambda e, y1=y1, g=g: e.copy(out=yzb[:, g * 512:(g + 1) * 512], in_=y1[:]), reads=[dy1], writes=[d_yzb[g]])
                pv = prev[:, g, :].rearrange("p (h d) -> p h d", d=64)
                P.op("dve", lambda e, pv=pv, g=g, sub=sub: e.tensor_tensor(out=pv, in0=pv, in1=ex3[:, sub, 64 + g * 8:64 + (g + 1) * 8].unsqueeze(2).to_broadcast([128, 8, 64]), op=ALU.mult),
                     reads=[d_prev[g], d_ex3[sub]], writes=[d_prev[g]])
                P.op("dve", lambda e, st=st, g=g: e.tensor_tensor(out=prev[:, g, :], in0=st[:], in1=prev[:, g, :], op=ALU.add), reads=[d_st, d_prev[g]], writes=[d_prev[g]])
                P.op("pool", lambda e, g=g: e.tensor_copy(out=prevb[:, g, :], in_=prev[:, g, :]), reads=[d_prev[g]], writes=[d_prevb[g]])
            P.op("dve", lambda e, sub=sub: e.tensor_reduce(out=sst[:, sub:sub + 1], in_=ssq[:, sub, :], axis=AX.X, op=ALU.add), reads=[d_ssq[sub]], writes=[d_sst])
            for cg in range(4):
                a, d_a = bk.next()
                av = a[:].bitcast(BF16)[:, 0:512]
                for j in range(4):
                    ch = cg * 4 + j
                    P.op("pe", lambda e, av=av, ch=ch, j=j: e.transpose(out=av[:, j * 128:(j + 1) * 128], in_=yzb[:, ch * 128:(ch + 1) * 128], identity=identb[:]),
                         reads=[d_yzb[cg], d_identb], writes=[d_a], inc=(j == 3))
                for j in range(4):
                    ch = cg * 4 + j
                    eng = "dve" if j % 2 == 0 else "pool"
                    if eng == "dve":
                        P.op("dve", lambda e, av=av, ch=ch, j=j, t0=t0: e.tensor_scalar(out=yT[:, ch, t0:t0 + 128], in0=av[:, j * 128:(j + 1) * 128], scalar1=gno[:, ch:ch + 1], scalar2=None, op0=ALU.mult),
                             reads=[d_a, d_gno], writes=[d_yT])
                    else:
                        P.op("act", lambda e, av=av, ch=ch, j=j, t0=t0: e.activation(out=yT[:, ch, t0:t0 + 128], in_=av[:, j * 128:(j + 1) * 128], func=AF.Copy, scale=gno[:, ch:ch + 1]),
                             reads=[d_a, d_gno], writes=[d_yT])
        P.dma(A["yT"][:, :, c0:c0 + TT].rearrange("c p t -> p c t"), yT[:], reads=[d_yT], final=True, queue="act")
        P.dma(A["ss"][c0:c0 + TT].rearrange("(s p) -> p s", p=128), sst[:], reads=[d_sst], final=True, queue="act", allow_slow_non_contiguous=True)


def build_m1():
    nc = new_nc()
    A = {}
    A["hnT"] = din(nc, "hnT", [KC, 128, S], BF16)
    A["win"] = din(nc, "win", [10, 128, 8192], BF16)
    A["wdt"] = din(nc, "wdt", [128, KC * 32], BF16)
    A["convw"] = din(nc, "convw", [128, 24 * 4], F32)
    A["convb"] = din(nc, "convb", [128, 24], F32)
    A["dtbias"] = din(nc, "dtbias", [128, 32], F32)
    A["alog"] = din(nc, "alog", [128, 32], F32)
    A["dskip"] = din(nc, "dskip", [128, 2048], F32)
    A["gnorm"] = din(nc, "gnorm", [128, KC], F32)
    A["triI"] = din(nc, "triI", [128, 128], F32)
    A["triS"] = din(nc, "triS", [128, 128], F32)
    A["ones32"] = din(nc, "ones32", [128, 128], F32)
    A["identb"] = din(nc, "identb", [128, 128], BF16)
    A["yT"] = dout(nc, "yT", [KC, 128, S], BF16)
    A["ss"] = dout(nc, "ss", [S], F32)
    P = Prog(nc)
    C = Common(P)
    emit_m1(P, C, A)
    P.emit()
    return nc


def panel(W, C):
    K, N = W.shape
    return np.ascontiguousarray(W.reshape(K // 128, 128, N // C, C).transpose(2, 1, 0, 3)).reshape(N // C, 128, (K // 128) * C)


def fm(vec, nch):
    return np.ascontiguousarray(vec.reshape(nch, 128).T)


def _m0_w(w_in, hg):
    q = w_in[:, hg * 512:hg * 512 + 512]
    k = w_in[:, 1024 + hg * 512:1024 + hg * 512 + 512]
    v = w_in[:, 2048 + hg * 512:2048 + hg * 512 + 512]
    gqk = np.concatenate([w_in[:, 3072 + hg * 256:3072 + hg * 256 + 256], w_in[:, 3584 + hg * 256:3584 + hg * 256 + 256]], axis=1)
    gv_ = w_in[:, 4096 + hg * 512:4096 + hg * 512 + 512]
    go = w_in[:, 5136 + hg * 512:5136 + hg * 512 + 512]
    win = np.concatenate([panel(m, 512) for m in (q, k, v, gqk, gv_, go)], axis=0)
    wlr = np.ascontiguousarray(w_in[:, 5120:5136].reshape(16, 128, 16).transpose(1, 0, 2)).reshape(128, 256)
    return win, wlr


def _m1_w(w_in, hg):
    zc = w_in[:, hg * 2048:(hg + 1) * 2048]
    xc = w_in[:, 4096 + hg * 2048:4096 + (hg + 1) * 2048]
    Bc = w_in[:, 8192 + hg * 512:8192 + (hg + 1) * 512]
    Cc = w_in[:, 9216 + hg * 512:9216 + (hg + 1) * 512]
    dtc = w_in[:, 10240 + hg * 32:10240 + (hg + 1) * 32]
    win = np.concatenate([panel(zc, 512), panel(xc, 512), panel(Bc, 512), panel(Cc, 512)], axis=0)
    wdt = np.ascontiguousarray(dtc.reshape(16, 128, 32).transpose(1, 0, 2)).reshape(128, 512)
    return win, wdt


def _r_w(w_o, w_up, w_down, w_gate, w_proj):
    KO = w_o.shape[0] // 128
    return {"wo": panel(w_o, 8192 // KO), "wup": panel(w_up, 512), "wdn": panel(w_down, 128), "wg": panel(w_gate, 512),
            "wpj": np.ascontiguousarray(w_proj.reshape(2, 128, 2048).transpose(1, 0, 2)).reshape(128, 4096)}


_NC_CACHE = {}


def _get_nc(key, fn):
    if key not in _NC_CACHE:
        _NC_CACHE[key] = fn()
    return _NC_CACHE[key]


def _run(nc, in_maps):
    res = run_bass_kernel_spmd(nc, in_maps, core_ids=list(range(8)))
    return res.results


def _cast_all(arrs):
    names = list(arrs)
    flats = [np.ascontiguousarray(arrs[n], dtype=np.float32).reshape(128, -1) for n in names]
    sizes = [f.shape[1] for f in flats]
    M = sum(sizes)
    Mp = ((M + 8 * 8 - 1) // 64) * 64
    big = np.zeros((128, Mp), np.float32)
    o = 0
    for f in flats:
        big[:, o:o + f.shape[1]] = f
        o += f.shape[1]
    per = Mp // 8
    nc = _get_nc(("cast", per), lambda: build_cast(per))
    res = _run(nc, [{"w32": np.ascontiguousarray(big[:, c * per:(c + 1) * per])} for c in range(8)])
    out16 = np.concatenate([np.asarray(r["w16"]) for r in res], axis=1)
    out = {}
    o = 0
    for n, sz in zip(names, sizes):
        out[n] = np.ascontiguousarray(out16[:, o:o + sz]).reshape(arrs[n].shape)
        o += sz
    return out


def kernel(x, p, norm_mix, norm_mlp, ab_w_in, ab_w_gate_up, ab_b_gate, ab_gla_norm, ab_w_out,
           ssd_w_in, ssd_conv_w, ssd_conv_b, ssd_dt_bias, ssd_a_log, ssd_d, ssd_norm, ssd_w_out,
           mlp_w_up, mlp_w_down, ple_w_proj, ple_w_gate, final_norm):
    global S, NTT
    S = 4096
    NTT = S // TT
    f32 = lambda a: np.asarray(a, dtype=np.float32)
    x = f32(x); p = f32(p); norm_mix = f32(norm_mix); norm_mlp = f32(norm_mlp)
    ab_w_in = f32(ab_w_in)[0]; ab_w_gate_up = f32(ab_w_gate_up)[0]; ab_b_gate = f32(ab_b_gate)[0]
    ab_gla_norm = f32(ab_gla_norm)[0]; ab_w_out = f32(ab_w_out)[0]
    ssd_w_in = f32(ssd_w_in)[0]; ssd_conv_w = f32(ssd_conv_w)[0]; ssd_conv_b = f32(ssd_conv_b)[0]
    ssd_dt_bias = f32(ssd_dt_bias)[0]; ssd_a_log = f32(ssd_a_log)[0]; ssd_d = f32(ssd_d)[0]
    ssd_norm = f32(ssd_norm)[0]; ssd_w_out = f32(ssd_w_out)[0]
    mlp_w_up = f32(mlp_w_up); mlp_w_down = f32(mlp_w_down); ple_w_proj = f32(ple_w_proj); ple_w_gate = f32(ple_w_gate)
    final_norm = f32(final_norm)
    B, NT = 4, 2048
    W = {}
    for hg in range(2):
        W[f"m0win{hg}"], W[f"m0wlr{hg}"] = _m0_w(ab_w_in, hg)
        W[f"m0wgu{hg}"] = np.ascontiguousarray(ab_w_gate_up[:, hg * 256:(hg + 1) * 256])
        W[f"m1win{hg}"], W[f"m1wdt{hg}"] = _m1_w(ssd_w_in, hg)
    r0 = _r_w(ab_w_out, mlp_w_up[0], mlp_w_down[0], ple_w_gate[0], ple_w_proj[0])
    r1 = _r_w(ssd_w_out, mlp_w_up[1], mlp_w_down[1], ple_w_gate[1], ple_w_proj[1])
    for k, v in r0.items():
        W["r0" + k] = v
    for k, v in r1.items():
        W["r1" + k] = v
    W16 = _cast_all(W)
    del W
    ident = np.eye(128, dtype=np.float32)
    identb = ident.astype(NPBF16)
    xs = x.reshape(8, NT, D)
    nc = _get_nc(("p0", NT), lambda: build_p0(NT))
    g0 = fm(norm_mix[0], 16)
    res = _run(nc, [{"x": np.ascontiguousarray(xs[c]), "g": g0, "ident": ident} for c in range(8)])
    xT = [np.asarray(r["xT"]) for r in res]
    hnT = [np.asarray(r["hnT"]) for r in res]
    tri = (np.arange(128)[:, None] >= np.arange(128)[None, :]).astype(np.float32).astype(NPBF16)
    ss_, tt_ = np.arange(128)[:, None], np.arange(128)[None, :]
    m2 = ((ss_ // 64 == tt_ // 64) & (ss_ <= tt_)).astype(np.float32)
    rmask = np.tile((np.arange(512) % 64 != 0).astype(np.float32)[None, :], (128, 1))
    nc = _get_nc("m0", build_m0)
    ims = []
    for c in range(8):
        b, hg = c // 2, c % 2
        hb = np.concatenate([hnT[2 * b], hnT[2 * b + 1]], axis=2)
        ims.append({"hnT": hb, "win": W16[f"m0win{hg}"], "wlr": W16[f"m0wlr{hg}"], "wgu": W16[f"m0wgu{hg}"],
                    "bgate": np.ascontiguousarray(ab_b_gate[hg * 256:(hg + 1) * 256].reshape(2, 128).T),
                    "gnorm": ab_gla_norm.reshape(128, 1).copy(), "tri": tri, "m2": m2, "rmask": rmask, "identb": identb})
    res = _run(nc, ims)
    o0 = [np.asarray(r["oT"]) for r in res]
    nc = _get_nc(("r", 16, False), lambda: build_r(NT, 16, False))
    ims = []
    for c in range(8):
        b, half = c // 2, c % 2
        a0, a1 = o0[2 * b], o0[2 * b + 1]
        of = np.concatenate([a0[0:4], a1[0:4], a0[4:8], a1[4:8]], axis=0)[:, :, half * NT:(half + 1) * NT]
        pT = np.ascontiguousarray(p[0, b, half * NT:(half + 1) * NT].T.reshape(2, 128, NT))
        ims.append({"hT": xT[c], "oT": np.ascontiguousarray(of), "pT": pT, "g_mlp": fm(norm_mlp[0], 16), "g_nxt": fm(norm_mix[1], 16),
                    "wo": W16["r0wo"], "wup": W16["r0wup"], "wdn": W16["r0wdn"], "wg": W16["r0wg"], "wpj": W16["r0wpj"]})
    res = _run(nc, ims)
    h1T = [np.asarray(r["hT_out"]) for r in res]
    hn1T = [np.asarray(r["hnT_out"]) for r in res]
    nc = _get_nc("m1", build_m1)
    j_, l_ = np.arange(128)[:, None], np.arange(128)[None, :]
    triI = (j_ <= l_).astype(np.float32)
    triS = (j_ > l_).astype(np.float32)
    ones32 = np.ones((128, 128), np.float32)
    ims = []
    for c in range(8):
        b, hg = c // 2, c % 2
        hb = np.concatenate([hn1T[2 * b], hn1T[2 * b + 1]], axis=2)
        idx = np.concatenate([hg * 2048 + np.arange(2048), 4096 + hg * 512 + np.arange(512), 5120 + hg * 512 + np.arange(512)])
        cw = ssd_conv_w[:, idx]
        hs = slice(hg * 32, (hg + 1) * 32)
        ims.append({"hnT": hb, "win": W16[f"m1win{hg}"], "wdt": W16[f"m1wdt{hg}"],
                    "convw": np.ascontiguousarray(cw.reshape(4, 24, 128).transpose(2, 1, 0)).reshape(128, 96),
                    "convb": fm(ssd_conv_b[idx], 24),
                    "dtbias": np.tile(ssd_dt_bias[hs][None, :], (128, 1)), "alog": np.tile(ssd_a_log[hs][None, :], (128, 1)),
                    "dskip": np.tile(np.repeat(ssd_d[hs], 64)[None, :], (128, 1)),
                    "gnorm": fm(ssd_norm[hg * 2048:(hg + 1) * 2048], 16),
                    "triI": triI, "triS": triS, "ones32": ones32, "identb": identb})
    res = _run(nc, ims)
    y1 = [np.asarray(r["yT"]) for r in res]
    ss1 = [np.asarray(r["ss"]) for r in res]
    nc = _get_nc(("r", 32, True), lambda: build_r(NT, 32, True))
    ims = []
    for c in range(8):
        b, half = c // 2, c % 2
        sl = slice(half * NT, (half + 1) * NT)
        of = np.concatenate([y1[2 * b], y1[2 * b + 1]], axis=0)[:, :, sl]
        ssb = np.stack([ss1[2 * b][sl], ss1[2 * b + 1][sl]], axis=0)
        pT = np.ascontiguousarray(p[1, b, sl].T.reshape(2, 128, NT))
        ims.append({"hT": h1T[c], "oT": np.ascontiguousarray(of), "pT": pT, "ss": np.ascontiguousarray(ssb), "ident": ident,
                    "g_mlp": fm(norm_mlp[1], 16), "g_nxt": fm(final_norm, 16),
                    "wo": W16["r1wo"], "wup": W16["r1wup"], "wdn": W16["r1wdn"], "wg": W16["r1wg"], "wpj": W16["r1wpj"]})
    res = _run(nc, ims)
    out = np.concatenate([np.asarray(r["out"]) for r in res], axis=0).reshape(B, 4096, D)
    return out.astype(np.float32)
```
